# Optimizing a Trainium2 kernel written in Bass

```python
import math
import jax
import jax.numpy as jnp
from jax import lax
import numpy as np

D_MODEL = 2048
BATCH = 4
SEQ = 2048
DEPTH = 1

ATTN_HEAD_DIM = 64
D_ATTN = D_MODEL // 2
N_ATTN_HEADS = D_ATTN // ATTN_HEAD_DIM
ROT_DIM = ATTN_HEAD_DIM // 4
ROPE_THETA = 500000.0
DILATED_PATTERNS = ((128, 1), (512, 4), (2048, 16))
ATTN_BLOCK = 128
SSM_HEAD_DIM = 64
D_SSM = D_MODEL // 2
N_SSM_HEADS = D_SSM // SSM_HEAD_DIM
SSM_GROUPS = 4
SSM_STATE = 128
CONV_WIDTH = 4
SSD_CHUNK = 128
D_CONV = D_SSM + 2 * SSM_GROUPS * SSM_STATE
DT_MIN = 1e-3
DT_MAX = 1e-1
D_MIX = D_ATTN + D_SSM
D_IN = 3 * D_ATTN + D_SSM + D_CONV + N_SSM_HEADS
N_MEM = 256
N_CROSS_HEADS = 4
CROSS_HEAD_DIM = 128
D_CROSS = N_CROSS_HEADS * CROSS_HEAD_DIM
D_FF = 4 * D_MODEL
EPS = 1e-6

kernel_name = "hymba_ssd_dilated_attn_memxattn_sqrelu"


def rms_norm(x, g):
    xf = x.astype(jnp.float32)
    y = xf * lax.rsqrt(jnp.mean(xf * xf, axis=-1, keepdims=True) + EPS)
    return (y * g.astype(jnp.float32)).astype(x.dtype)


def partial_rope(x, positions):
    half = ROT_DIM // 2
    inv_freq = ROPE_THETA ** (-2.0 * jnp.arange(half, dtype=jnp.float32) / ROT_DIM)
    ang = positions.astype(jnp.float32)[..., None] * inv_freq
    cos = jnp.cos(ang)[:, :, None, :]
    sin = jnp.sin(ang)[:, :, None, :]
    xf = x.astype(jnp.float32)
    x1 = xf[..., :half]
    x2 = xf[..., half:ROT_DIM]
    out = jnp.concatenate([x1 * cos - x2 * sin, x2 * cos + x1 * sin, xf[..., ROT_DIM:]], axis=-1)
    return out.astype(x.dtype)


def dilated_window_branch(q, k, v, window, dilation):
    bsz, s_len, n_h, hd = q.shape
    steps = window // dilation
    span = dilation * ATTN_BLOCK
    s_pad = -(-s_len // span) * span
    nb = s_pad // span

    def to_blocks(t):
        t = jnp.pad(t, ((0, 0), (0, s_pad - s_len), (0, 0), (0, 0)))
        return t.reshape(bsz, nb, ATTN_BLOCK, dilation, n_h, hd)

    qb, kb, vb = to_blocks(q), to_blocks(k), to_blocks(v)

    def band(t):
        prev = jnp.concatenate([jnp.zeros_like(t[:, :1]), t[:, :-1]], axis=1)
        return jnp.concatenate([prev, t], axis=2)

    kband, vband = band(kb), band(vb)
    s = jnp.einsum('bnqrhd,bnkrhd->bnrhqk', qb, kband)
    qi = jnp.arange(ATTN_BLOCK)[:, None]
    kj = jnp.arange(2 * ATTN_BLOCK)[None, :]
    dist = qi + ATTN_BLOCK - kj
    in_window = (dist >= 0) & (dist <= steps)
    has_prev = (jnp.arange(nb)[:, None, None] > 0) | (kj[None] >= ATTN_BLOCK)
    mask = in_window[None] & has_prev
    s = jnp.where(mask[None, :, None, None], s, -jnp.inf)
    m = jnp.max(s, axis=-1, keepdims=True)
    p = jnp.exp(s - m)
    denom = jnp.sum(p, axis=-1)
    lse = m[..., 0] + jnp.log(denom)
    o = jnp.einsum('bnrhqk,bnkrhd->bnqrhd', p, vband)
    o = o / jnp.transpose(denom, (0, 1, 4, 2, 3))[..., None]
    o = o.reshape(bsz, s_pad, n_h, hd)[:, :s_len]
    lse = jnp.transpose(lse, (0, 1, 4, 2, 3)).reshape(bsz, s_pad, n_h)[:, :s_len]
    return o, lse


def dilated_attention(q, k, v, positions, g_q, g_k):
    q = partial_rope(rms_norm(q, g_q), positions).astype(jnp.float32) * (ATTN_HEAD_DIM ** -0.5)
    k = partial_rope(rms_norm(k, g_k), positions).astype(jnp.float32)
    v = v.astype(jnp.float32)
    outs, lses = [], []
    for window, dilation in DILATED_PATTERNS:
        o, l = dilated_window_branch(q, k, v, window, dilation)
        outs.append(o)
        lses.append(l)
    wts = jax.nn.softmax(jnp.stack(lses, axis=0), axis=0)
    return jnp.einsum('gbsh,gbshd->bshd', wts, jnp.stack(outs, axis=0))


def ssd_chunked(x, dt, a, b_mat, c_mat):
    bsz, l_len, n_h, p_dim = x.shape
    g, n = b_mat.shape[2], b_mat.shape[3]
    hg = n_h // g
    nc = l_len // SSD_CHUNK
    q = SSD_CHUNK
    xc = x.reshape(bsz, nc, q, g, hg, p_dim)
    dtc = dt.reshape(bsz, nc, q, g, hg)
    bc = b_mat.reshape(bsz, nc, q, g, n)
    cc = c_mat.reshape(bsz, nc, q, g, n)
    a_cs = jnp.cumsum(dtc * a.reshape(g, hg), axis=2)
    seg = a_cs[:, :, :, None] - a_cs[:, :, None, :]
    causal = jnp.tril(jnp.ones((q, q), dtype=bool))[:, :, None, None]
    l_mat = jnp.exp(jnp.where(causal, seg, -jnp.inf))
    cb = jnp.einsum('bclgn,bcsgn->bclsg', cc, bc)
    w = cb[..., None] * l_mat * dtc[:, :, None]
    y_diag = jnp.einsum('bclsgh,bcsghp->bclghp', w, xc)
    decay_states = jnp.exp(a_cs[:, :, -1:] - a_cs)
    states = jnp.einsum('bcsgn,bcsgh,bcsghp->bcghpn', bc, decay_states * dtc, xc)
    chunk_decay = jnp.exp(a_cs[:, :, -1])

    def step(h, inp):
        s_c, a_c = inp
        return h * a_c[..., None, None] + s_c, h

    h0 = jnp.zeros((bsz, g, hg, p_dim, n), jnp.float32)
    _, prev = lax.scan(step, h0, (jnp.moveaxis(states, 1, 0), jnp.moveaxis(chunk_decay, 1, 0)))
    prev = jnp.moveaxis(prev, 0, 1)
    y_off = jnp.einsum('bclgn,bcghpn->bclghp', cc, prev) * jnp.exp(a_cs)[..., None]
    return (y_diag + y_off).reshape(bsz, l_len, n_h, p_dim)


def ssd_mixer(z, xbc, dt_raw, conv_w, conv_b, dt_bias, a_log, d_skip, g_out):
    bsz, l_len, _ = xbc.shape
    xbc = lax.conv_general_dilated(
        xbc, conv_w.astype(xbc.dtype)[:, None, :], window_strides=(1,),
        padding=[(CONV_WIDTH - 1, 0)], dimension_numbers=('NWC', 'WIO', 'NWC'),
        feature_group_count=D_CONV) + conv_b.astype(xbc.dtype)
    xbc = jax.nn.silu(xbc)
    xs = xbc[..., :D_SSM].astype(jnp.float32).reshape(bsz, l_len, N_SSM_HEADS, SSM_HEAD_DIM)
    b_mat = xbc[..., D_SSM:D_SSM + SSM_GROUPS * SSM_STATE].astype(jnp.float32).reshape(bsz, l_len, SSM_GROUPS, SSM_STATE)
    c_mat = xbc[..., D_SSM + SSM_GROUPS * SSM_STATE:].astype(jnp.float32).reshape(bsz, l_len, SSM_GROUPS, SSM_STATE)
    dt = jax.nn.softplus(dt_raw.astype(jnp.float32) + dt_bias.astype(jnp.float32))
    a = -jnp.exp(a_log.astype(jnp.float32))
    y = ssd_chunked(xs, dt, a, b_mat, c_mat) + d_skip.astype(jnp.float32)[:, None] * xs
    y = y.reshape(bsz, l_len, D_SSM) * jax.nn.silu(z.astype(jnp.float32))
    y = rms_norm(y.reshape(bsz, l_len, SSM_GROUPS, D_SSM // SSM_GROUPS),
                 g_out.reshape(SSM_GROUPS, D_SSM // SSM_GROUPS))
    return y.reshape(bsz, l_len, D_SSM).astype(z.dtype)


def cross_attention(h, mem_h, w_q, w_kv, w_o, g_q, g_k):
    bsz, s_len, _ = h.shape
    q = (h @ w_q).reshape(bsz, s_len, N_CROSS_HEADS, CROSS_HEAD_DIM)
    kv = (mem_h @ w_kv).reshape(bsz, -1, 2, N_CROSS_HEADS, CROSS_HEAD_DIM)
    k, v = kv[:, :, 0], kv[:, :, 1]
    q = rms_norm(q, g_q).astype(jnp.float32) * (CROSS_HEAD_DIM ** -0.5)
    k = rms_norm(k, g_k).astype(jnp.float32)
    p = jax.nn.softmax(jnp.einsum('bshd,bmhd->bhsm', q, k), axis=-1)
    o = jnp.einsum('bhsm,bmhd->bshd', p, v.astype(jnp.float32)).astype(h.dtype)
    return o.reshape(bsz, s_len, D_CROSS) @ w_o


def setup_inputs(seed: int = 0) -> dict:
    key = jax.random.key(seed)
    ks = jax.random.split(key, 26)
    f32 = jnp.float32

    def nrm(k, shape, scale):
        return jax.random.normal(k, shape, f32) * scale

    def gain(k, shape):
        return 1.0 + 0.02 * jax.random.normal(k, shape, f32)

    dt0 = jnp.exp(jax.random.uniform(ks[9], (DEPTH, N_SSM_HEADS), f32, math.log(DT_MIN), math.log(DT_MAX)))
    return {
        "x": nrm(ks[0], (BATCH, SEQ, D_MODEL), 1.0),
        "mem": nrm(ks[1], (BATCH, N_MEM, D_MODEL), 1.0),
        "positions": jnp.broadcast_to(jnp.arange(SEQ, dtype=jnp.int32), (BATCH, SEQ)),
        "g_mix": gain(ks[2], (DEPTH, D_MODEL)),
        "w_in": nrm(ks[3], (DEPTH, D_MODEL, D_IN), D_MODEL ** -0.5),
        "g_q": gain(ks[4], (DEPTH, ATTN_HEAD_DIM)),
        "g_k": gain(ks[5], (DEPTH, ATTN_HEAD_DIM)),
        "g_attn_out": gain(ks[6], (DEPTH, D_ATTN)),
        "conv_w": nrm(ks[7], (DEPTH, CONV_WIDTH, D_CONV), CONV_WIDTH ** -0.5),
        "conv_b": nrm(ks[8], (DEPTH, D_CONV), 0.02),
        "dt_bias": dt0 + jnp.log(-jnp.expm1(-dt0)),
        "a_log": jnp.log(jax.random.uniform(ks[10], (DEPTH, N_SSM_HEADS), f32, 1.0, 16.0)),
        "d_skip": 1.0 + 0.1 * jax.random.normal(ks[11], (DEPTH, N_SSM_HEADS), f32),
        "g_ssm_out": gain(ks[12], (DEPTH, D_SSM)),
        "w_out": nrm(ks[13], (DEPTH, D_MIX, D_MODEL), D_MIX ** -0.5),
        "g_cross": gain(ks[14], (DEPTH, D_MODEL)),
        "g_mem": gain(ks[15], (DEPTH, D_MODEL)),
        "w_cq": nrm(ks[16], (DEPTH, D_MODEL, D_CROSS), D_MODEL ** -0.5),
        "w_ckv": nrm(ks[17], (DEPTH, D_MODEL, 2 * D_CROSS), D_MODEL ** -0.5),
        "g_cq": gain(ks[18], (DEPTH, CROSS_HEAD_DIM)),
        "g_ck": gain(ks[19], (DEPTH, CROSS_HEAD_DIM)),
        "w_co": nrm(ks[20], (DEPTH, D_CROSS, D_MODEL), D_CROSS ** -0.5),
        "g_mlp": gain(ks[21], (DEPTH, D_MODEL)),
        "w_up": nrm(ks[22], (DEPTH, D_MODEL, D_FF), D_MODEL ** -0.5),
        "w_down": nrm(ks[23], (DEPTH, D_FF, D_MODEL), D_FF ** -0.5),
    }


def reference(x, mem, positions, g_mix, w_in, g_q, g_k, g_attn_out, conv_w, conv_b, dt_bias,
              a_log, d_skip, g_ssm_out, w_out, g_cross, g_mem, w_cq, w_ckv, g_cq, g_ck, w_co,
              g_mlp, w_up, w_down):
    bsz, s_len, _ = x.shape
    splits = np.cumsum([D_ATTN, D_ATTN, D_ATTN, D_SSM, D_CONV]).tolist()
    for i in range(DEPTH):
        h = rms_norm(x, g_mix[i])
        q, k, v, z, xbc, dt_raw = jnp.split(h @ w_in[i], splits, axis=-1)
        shp = (bsz, s_len, N_ATTN_HEADS, ATTN_HEAD_DIM)
        attn = dilated_attention(q.reshape(shp), k.reshape(shp), v.reshape(shp), positions, g_q[i], g_k[i])
        attn = rms_norm(attn.reshape(bsz, s_len, D_ATTN), g_attn_out[i]).astype(x.dtype)
        ssm = ssd_mixer(z, xbc, dt_raw, conv_w[i], conv_b[i], dt_bias[i], a_log[i], d_skip[i], g_ssm_out[i])
        x = x + jnp.concatenate([attn, ssm], axis=-1) @ w_out[i]
        x = x + cross_attention(rms_norm(x, g_cross[i]), rms_norm(mem, g_mem[i]),
                                w_cq[i], w_ckv[i], w_co[i], g_cq[i], g_ck[i])
        hm = rms_norm(x, g_mlp[i])
        x = x + jnp.square(jax.nn.relu(hm @ w_up[i])) @ w_down[i]
    return x
```

```python
import numpy as np
import concourse.bass as bass
import concourse.mybir as mybir
from contextlib import ExitStack

F32 = mybir.dt.float32
BF16 = mybir.dt.bfloat16
I32 = mybir.dt.int32
AF = mybir.ActivationFunctionType
ALU = mybir.AluOpType
AX = mybir.AxisListType

ENGS = ("pe", "act", "dve", "pool", "sp")
ND_SEM = 12


class Op:
    __slots__ = ("eng", "fn", "deps", "is_dma", "signal", "idx", "semval", "dma_j", "name", "own_sem")


class Prog:
    def __init__(self, nc):
        self.nc = nc
        self.ops = {e: [] for e in ENGS}
        self.all = []
        self.hist = {}
        self.n_dma = {e: 0 for e in ENGS}
        self.dma_ops = {e: [] for e in ENGS}
        self.stack = ExitStack()
        self.out_tokens = []

    def sb(self, name, shape, dt):
        return self.stack.enter_context(self.nc.sbuf_tensor(name, list(shape), dt))

    def ps(self, name, shape, dt):
        return self.stack.enter_context(self.nc.psum_tensor(name, list(shape), dt))

    @staticmethod
    def box(ap):
        t = ap.tensor
        cls = type(t).__name__
        if "PSum" in cls:
            return (t.name, 0, 128, 0, 1 << 30, True)
        if "SB" not in cls:
            return None
        steps = ap.ap
        pstride, npart = steps[0]
        off = ap.offset
        es = mybir.dt.size(ap.dtype)
        if pstride == 0:
            pstride = 1 << 40
        p0 = off // pstride
        f0 = (off % pstride) * es
        ext = 0
        for s, c in steps[1:]:
            ext += (c - 1) * abs(s)
        return (t.name, p0, p0 + npart, f0, f0 + (ext + 1) * es, False)

    def _track(self, op, reads, writes):
        deps = set()
        for kind, aps in (("r", reads), ("w", writes)):
            for ap in aps:
                b = self.box(ap)
                if b is None:
                    continue
                name, p0, p1, f0, f1, is_ps = b
                k = "w" if is_ps else kind
                h = self.hist.setdefault(name, [])
                newh = []
                for ent in h:
                    (_, q0, q1, g0, g1, _), ek, eeng, eop = ent
                    ov = (q0 < p1 and p0 < q1 and g0 < f1 and f0 < g1)
                    if ov and (k == "w" or ek == "w"):
                        if eop is not op:
                            deps.add(eop)
                    if k == "w" and ov and q0 >= p0 and q1 <= p1 and g0 >= f0 and g1 <= f1 and eop is not op:
                        continue
                    if (k == ek and eeng == op.eng and not op.is_dma and (q0, q1, g0, g1) == (p0, p1, f0, f1)):
                        continue
                    newh.append(ent)
                newh.append((b, k, op.eng, op))
                self.hist[name] = newh
        return deps

    def op(self, eng, fn, reads=(), writes=(), name=""):
        o = Op()
        o.eng = eng; o.fn = fn; o.is_dma = False; o.signal = False; o.own_sem = False
        o.semval = None; o.dma_j = None; o.name = name
        o.deps = self._track(o, reads, writes)
        self.ops[eng].append(o)
        self.all.append(o)
        return o

    def dma(self, eng, out, in_, name="", is_output=False):
        o = Op()
        o.eng = eng; o.is_dma = True; o.signal = True; o.name = name; o.own_sem = False
        o.fn = lambda e: e.dma_start(out=out, in_=in_)
        o.dma_j = self.n_dma[eng]
        self.n_dma[eng] += 1
        o.deps = self._track(o, [in_], [out])
        if o.dma_j >= ND_SEM:
            o.deps.add(self.dma_ops[eng][o.dma_j - ND_SEM])
        self.dma_ops[eng].append(o)
        self.ops[eng].append(o)
        self.all.append(o)
        if is_output:
            self.out_tokens.append(o)
        return o

    def collective(self, fn, deps=(), name="cc"):
        o = Op()
        o.eng = "pool"; o.fn = fn; o.is_dma = True; o.signal = True
        o.semval = None; o.dma_j = None; o.name = name; o.own_sem = True
        o.deps = set(deps)
        self.ops["pool"].append(o)
        self.all.append(o)
        return o

    def wait_all(self, eng, tokens):
        o = Op()
        o.eng = eng; o.fn = None; o.is_dma = False; o.signal = False; o.own_sem = False
        o.semval = None; o.dma_j = None; o.name = "waitall"
        o.deps = set(tokens)
        self.ops[eng].append(o)
        self.all.append(o)
        return o

    def emit(self):
        nc = self.nc
        for o in self.all:
            for d in o.deps:
                if d.is_dma:
                    continue
                if d.eng == "pe" and o.eng == "pe":
                    continue
                d.signal = True
        sems = {e: self.stack.enter_context(nc.semaphore("s_" + e)) for e in ENGS}
        dsems = {e: [self.stack.enter_context(nc.semaphore("d_%s_%d" % (e, i))) for i in range(ND_SEM)]
                 for e in ENGS if self.n_dma[e] > 0}
        for e in ENGS:
            c = 0
            for o in self.ops[e]:
                if getattr(o, "own_sem", False):
                    o.semval = (self.stack.enter_context(nc.semaphore("cc_%d" % id(o))), 1)
                elif o.is_dma:
                    o.semval = (dsems[e][o.dma_j % ND_SEM], 16 * (o.dma_j // ND_SEM + 1))
                elif o.signal:
                    c += 1
                    o.semval = (sems[e], c)
        self.stats = {e: len(self.ops[e]) for e in ENGS}
        nwaits = {e: 0 for e in ENGS}

        def replay(e, eng):
            seen = {}
            for o in self.ops[e]:
                need = {}
                for d in o.deps:
                    if (not d.is_dma) and d.eng == "pe" and e == "pe":
                        continue
                    s, v = d.semval
                    key = id(s)
                    if seen.get(key, 0) >= v:
                        continue
                    if key not in need or need[key][1] < v:
                        need[key] = (s, v)
                for key, (s, v) in need.items():
                    eng.wait_ge(s, v)
                    seen[key] = v
                    nwaits[e] += 1
                if o.fn is not None:
                    ins = o.fn(eng)
                    if getattr(o, "own_sem", False):
                        ins.then_inc(o.semval[0])
                    elif o.is_dma:
                        ins.then_inc(o.semval[0], 16)
                    elif o.signal:
                        ins.then_inc(o.semval[0], 1)

        with nc.Block() as block:
            @block.tensor
            def _(eng):
                replay("pe", eng)

            @block.scalar
            def _(eng):
                replay("act", eng)

            @block.vector
            def _(eng):
                replay("dve", eng)

            @block.gpsimd
            def _(eng):
                replay("pool", eng)

            @block.sync
            def _(eng):
                replay("sp", eng)
        self.nwaits = nwaits
        self.stack.close()


def _isap(x):
    return hasattr(x, "tensor") and hasattr(x, "ap")


class P2(Prog):
    def mm(self, out, lhsT, rhs, start=True, stop=True, **kw):
        return self.op("pe", lambda e: e.matmul(out, lhsT=lhsT, rhs=rhs, start=start, stop=stop, **kw),
                       [lhsT, rhs], [out])

    def tr(self, out, in_, ident):
        return self.op("pe", lambda e: e.transpose(out=out, in_=in_, identity=ident), [in_, ident], [out])

    def act(self, out, in_, func, bias=None, scale=None, accum_out=None, eng="act"):
        kw = {}
        rd = [in_]
        wr = [out]
        if bias is not None:
            kw["bias"] = bias
            if _isap(bias): rd.append(bias)
        if scale is not None:
            kw["scale"] = scale
            if _isap(scale): rd.append(scale)
        if accum_out is not None:
            kw["accum_out"] = accum_out
            wr.append(accum_out)
        return self.op(eng, lambda e: e.activation(out=out, in_=in_, func=func, **kw), rd, wr)

    def tt(self, eng, out, in0, in1, op):
        return self.op(eng, lambda e: e.tensor_tensor(out=out, in0=in0, in1=in1, op=op), [in0, in1], [out])

    def ts(self, eng, out, in0, s1, s2=None, op0=None, op1=None, accum_out=None):
        rd = [in0]
        if _isap(s1): rd.append(s1)
        if _isap(s2): rd.append(s2)
        wr = [out]
        kw = {}
        if op1 is not None:
            kw["op1"] = op1
        if accum_out is not None:
            kw["accum_out"] = accum_out
            wr.append(accum_out)
        return self.op(eng, lambda e: e.tensor_scalar(out=out, in0=in0, scalar1=s1, scalar2=s2, op0=op0, **kw), rd, wr)

    def stt(self, eng, out, in0, scalar, in1, op0, op1):
        rd = [in0, in1]
        if _isap(scalar): rd.append(scalar)
        return self.op(eng, lambda e: e.scalar_tensor_tensor(out=out, in0=in0, scalar=scalar, in1=in1, op0=op0, op1=op1), rd, [out])

    def copy(self, eng, out, in_):
        return self.op(eng, lambda e: e.tensor_copy(out=out, in_=in_), [in_], [out])

    def memset(self, eng, out, val):
        return self.op(eng, lambda e: e.memset(out, val), [], [out])

    def reduce(self, eng, out, in_, op, axis=AX.X):
        return self.op(eng, lambda e: e.tensor_reduce(out=out, in_=in_, axis=axis, op=op), [in_], [out])

    def recip(self, out, in_):
        return self.op("dve", lambda e: e.reciprocal(out=out, in_=in_), [in_], [out])


import math

U8 = mybir.dt.uint8
EPS = 1e-6
NT = 1024
D = 2048
MOFF = 384
MW = 2432
ARENA_BYTES = 207 * 1024
NSLOT = 2
SLOT_BYTES = 16 * 1024

C_Q, C_K, C_V, C_Z, C_X, C_B, C_C, C_DT = 0, 1024, 2048, 3072, 4096, 5120, 5632, 6144

PP = {}
_o = 0
for _n, _w in (("gmix", 16), ("gcross", 16), ("gmlp", 16), ("gmem", 16), ("gq2", 1), ("gk2", 1),
               ("gattn", 8), ("gssm", 8), ("convw", 64), ("convb", 16), ("gcq", 1), ("gck", 1),
               ("invf", 1)):
    PP[_n] = (_o, _w)
    _o += _w
NPP = _o


def host_consts():
    c = {}
    c["ident"] = np.eye(128, dtype=np.float32)
    s = np.arange(128)
    c["tri"] = (s[:, None] <= s[None, :]).astype(np.float32)
    c["negmask"] = np.where(s[None, :] >= s[:, None], 0.0, -30000.0).astype(np.float32)
    bo = np.zeros((128, 128), np.float32)
    bo[:64, :64] = 1.0
    bo[64:, 64:] = 1.0
    c["blockones"] = bo
    rm = np.zeros((128, 128), np.float32)
    for m in range(128):
        dd = m % 64
        if dd < 8:
            rm[m + 8, m] = -1.0
        elif dd < 16:
            rm[m - 8, m] = 1.0
    c["rotm"] = rm
    ki = np.arange(128)[:, None]
    j = np.arange(MW)[None, :]
    dl = j - MOFF - ki
    cm = ((dl >= 0) & (dl <= 128)).astype(np.float32) + ((dl >= 0) & (dl <= 512) & (dl % 4 == 0)) \
        + ((dl >= 0) & (dl <= 2048) & (dl % 16 == 0))
    c["cmask"] = cm.astype(np.float32)
    op = np.zeros((128, 192), np.float32)
    op[:, 0:64] = 1.0
    op[:, 128:192] = 1.0
    c["onespad"] = op
    sw = np.zeros((128, 128), np.float32)
    for m in range(128):
        sw[(m + 64) % 128, m] = 1.0
    c["swap"] = sw
    return c


def host_pp(inp):
    pp = np.zeros((128, NPP), np.float32)

    def put(name, arr):
        o, w = PP[name]
        pp[:, o:o + w] = arr

    def chunks(v):
        return np.ascontiguousarray(v.reshape(-1, 128).T)
    put("gmix", chunks(inp["g_mix"][0]))
    put("gcross", chunks(inp["g_cross"][0]))
    put("gmlp", chunks(inp["g_mlp"][0]))
    put("gmem", chunks(inp["g_mem"][0]))
    put("gq2", np.tile(inp["g_q"][0], 2)[:, None])
    put("gk2", np.tile(inp["g_k"][0], 2)[:, None])
    put("gattn", chunks(inp["g_attn_out"][0]))
    put("gssm", chunks(inp["g_ssm_out"][0]))
    cw = inp["conv_w"][0]
    put("convw", np.ascontiguousarray(cw.reshape(4, 16, 128).transpose(2, 1, 0).reshape(128, 64)))
    put("convb", chunks(inp["conv_b"][0]))
    put("gcq", inp["g_cq"][0][:, None])
    put("gck", inp["g_ck"][0][:, None])
    half = 8
    inv = (500000.0 ** (-2.0 * np.arange(half, dtype=np.float32) / 16)).astype(np.float32)
    invp = np.zeros(128, np.float32)
    for p in range(128):
        dd = p % 64
        if dd < 16:
            invp[p] = inv[dd % 8]
    put("invf", invp[:, None])
    return pp


class Arena:
    def __init__(self, P):
        self.t = P.sb("arena", [128, ARENA_BYTES], U8)
        self.top = 0
        self.hi = 0

    def alloc(self, shape, dt):
        es = mybir.dt.size(dt)
        n = es
        for s in shape:
            n *= s
        off = (self.top + 63) // 64 * 64
        assert off + n <= ARENA_BYTES, ("arena overflow", off, n)
        self.top = off + n
        self.hi = max(self.hi, self.top)
        v = self.t[:, off:off + n].bitcast(dt)
        if len(shape) == 2:
            return v.rearrange("p (a b) -> p a b", a=shape[0])
        if len(shape) == 3:
            return v.rearrange("p (a b c) -> p a b c", a=shape[0], b=shape[1])
        return v

    def mark(self):
        return self.top

    def release(self, m):
        self.top = m


def build(P, d, dbg=None, stop=None, gather=False):
    dbg = dbg or {}
    A = Arena(P)
    PS = [P.ps("ps%d" % i, [128, 512], F32) for i in range(8)]
    PSB = [p[:].bitcast(BF16) for p in PS]
    gb_state = {"i": 0, "lst": [0, 1, 2, 3]}

    def gbank():
        lst = gb_state["lst"]
        b = lst[gb_state["i"] % len(lst)]
        gb_state["i"] += 1
        return b

    def tap(name, ap):
        if name in dbg:
            P.dma("sp", dbg[name], ap, is_output=True)

    WSHAPE = {"w_a": (2048, 4112), "w_b": (2048, 2048), "w_out": (2048, 2048), "w_ckv": (2048, 1024),
              "w_cq": (2048, 512), "w_co": (512, 2048), "w_up": (2048, 8192), "w_down": (8192, 2048)}
    WSRC = {}
    rg = [list(range(8))]

    def start_gather(names):
        for wn in names:
            if not gather:
                WSRC[wn] = (d[wn], None)
                continue
            nc = P.nc
            R_, C_ = WSHAPE[wn]
            bt = nc.dram_tensor(wn + "_bounce", [R_ // 8, C_], F32)
            ft = nc.dram_tensor(wn + "_full", [R_, C_], F32, addr_space="Shared")
            c1 = P.dma("pool", bt.ap(), d[wn])
            cc = P.collective(lambda e, bt=bt, ft=ft: e.collective_compute(
                "AllGather", mybir.AluOpType.bypass, replica_groups=rg,
                ins=[bt.ap().opt()], outs=[ft.ap().opt()]), deps=[c1], name="ag_" + wn)
            WSRC[wn] = (ft.ap(), cc)
    start_gather(["w_a"])

    def wsl(wn, r0, r1, c0, c1):
        ap, dep = WSRC[wn]
        return ap[r0:r1, c0:c1], dep
    WK = {"k": 0, "v": 1024}
    WA = {"x": 2048, "B": 3072, "C": 3584, "dt": 4096}
    WB = {"q": 0, "z": 1024}

    def cload(name, shape, dt_out, src=None):
        t = A.alloc(shape, dt_out)
        src = d[name] if src is None else src
        if dt_out == F32:
            P.dma("sp", t, src)
        else:
            P.dma("pool", t, src)
        return t

    ident_f = cload("ident", [128], F32)
    ident_b = cload("ident", [128], BF16)
    tri_f = cload("tri", [128], F32)
    negmask_f = cload("negmask", [128], F32)
    blockones_b = cload("blockones", [128], BF16)
    rotm_b = cload("rotm", [128], BF16)
    cmask_b = cload("cmask", [MW], BF16)
    swap_f = cload("swap", [128], F32)
    pp = cload("pp", [NPP], F32)
    flag = cload("flag", [1], F32)
    dtb_row = cload("dt_bias", [16], F32, d["dt_bias"].partition_broadcast(128))
    alog_row = cload("a_log", [16], F32, d["a_log"].partition_broadcast(128))
    dskip_row = cload("d_skip", [16], F32, d["d_skip"].partition_broadcast(128))
    gq_row = cload("g_q", [64], F32, d["g_q"].partition_broadcast(128))
    gk_row = cload("g_k", [64], F32, d["g_k"].partition_broadcast(128))
    gcq_row = cload("g_cq", [128], F32, d["g_cq"].partition_broadcast(128))
    gck_row = cload("g_ck", [128], F32, d["g_ck"].partition_broadcast(128))
    ones_b = A.alloc([128], BF16)
    P.memset("dve", ones_b, 1.0)
    ones_f = A.alloc([128], F32)
    P.memset("dve", ones_f, 1.0)
    eps_col = A.alloc([1], F32)
    P.memset("dve", eps_col, EPS)
    eps64_col = A.alloc([1], F32)
    P.memset("dve", eps64_col, 64.0 * EPS)
    eps128_col = A.alloc([1], F32)
    P.memset("dve", eps128_col, 128.0 * EPS)
    arow = A.alloc([16], F32)
    P.act(arow, alog_row, AF.Exp)
    P.ts("dve", arow, arow, -1.0, None, op0=ALU.mult)
    wdt_b = A.alloc([16, 16], BF16)
    _ap, _dep = wsl("w_a", 0, 2048, WA["dt"], WA["dt"] + 16)
    _o = P.dma("pool", wdt_b, _ap.rearrange("(c p) n -> p c n", p=128))
    if _dep is not None:
        _o.deps.add(_dep)

    def ppc(name, i=0, n=1):
        o, w = PP[name]
        return pp[:, o + i:o + i + n]

    def neg_bound(rowa, rowb, n, hd_scale):
        t = A.alloc([n], F32)
        m1 = A.alloc([1], F32)
        m2 = A.alloc([1], F32)
        P.tt("dve", t, rowa, rowa, ALU.mult)
        P.reduce("dve", m1, t, ALU.max)
        P.tt("dve", t, rowb, rowb, ALU.mult)
        P.reduce("dve", m2, t, ALU.max)
        P.tt("dve", m1, m1, m2, ALU.mult)
        P.act(m1, m1, AF.Sqrt, scale=hd_scale)
        nb = A.alloc([1], F32)
        P.ts("dve", nb, m1, -1.0, None, op0=ALU.mult)
        return nb
    negb_attn = neg_bound(gq_row, gk_row, 64, 64.0)
    negb_cross = neg_bound(gcq_row, gck_row, 128, 128.0)

    slots = [A.alloc([SLOT_BYTES // 2], BF16) for _ in range(NSLOT)]
    slot_i = {"i": 0}

    def load_w(srcdep, kch, ncols):
        src, dep = srcdep
        s = slots[slot_i["i"] % NSLOT]
        slot_i["i"] += 1
        v = s[:, 0:kch * ncols].rearrange("p (k n) -> p k n", k=kch)
        sv = src.rearrange("(k p) n -> p k n", p=128)
        nd = 4
        per = kch // nd
        for q in range(nd):
            o_ = P.dma("pool", v[:, q * per:(q + 1) * per, :], sv[:, q * per:(q + 1) * per, :])
            if dep is not None:
                o_.deps.add(dep)
        return v

    m_persist = A.mark()

    def norm_stream(xT_dram, g_name, hT, ntok=NT):
        m = A.mark()
        xv = xT_dram.rearrange("(c p) n -> p c n", p=128)
        xc = [A.alloc([ntok], F32) for _ in range(2)]
        sq = [A.alloc([ntok], BF16) for _ in range(2)]
        rstd = A.alloc([ntok], F32)
        tsz = min(512, ntok)
        nt = ntok // tsz
        banks = [4 + i for i in range(nt)]
        for c in range(16):
            P.dma("sp", xc[c % 2], xv[:, c, :])
            P.act(sq[c % 2], xc[c % 2], AF.Square)
            for t in range(nt):
                P.mm(PS[banks[t]][:, 0:tsz], ones_b, sq[c % 2][:, t * tsz:(t + 1) * tsz], start=(c == 0), stop=(c == 15))
        for t in range(nt):
            P.act(rstd[:, t * tsz:(t + 1) * tsz], PS[banks[t]][:, 0:tsz], AF.Sqrt, bias=eps_col[:, 0:1], scale=1.0 / D)
        P.recip(rstd, rstd)
        for c in range(16):
            P.dma("sp", xc[c % 2], xv[:, c, :])
            P.stt("dve", hT[:, c, :], xc[c % 2], ppc(g_name, c), rstd, ALU.mult, ALU.mult)
        A.release(m)

    def norm_resident(xT, g_name, hT, ntok=NT):
        m = A.mark()
        sq = [A.alloc([ntok], BF16) for _ in range(2)]
        rstd = A.alloc([ntok], F32)
        nt = ntok // 512
        banks = [4 + i for i in range(nt)]
        for c in range(16):
            P.act(sq[c % 2], xT[:, c, :], AF.Square)
            for t in range(nt):
                P.mm(PS[banks[t]][:, :], ones_b, sq[c % 2][:, t * 512:(t + 1) * 512], start=(c == 0), stop=(c == 15))
        for t in range(nt):
            P.act(rstd[:, t * 512:(t + 1) * 512], PS[banks[t]][:, :], AF.Sqrt, bias=eps_col[:, 0:1], scale=1.0 / D)
        P.recip(rstd, rstd)
        for c in range(16):
            P.stt("dve", hT[:, c, :], xT[:, c, :], ppc(g_name, c), rstd, ALU.mult, ALU.mult)
        A.release(m)

    def proj_fm(w, kch, hT, tiles, nblk, evac):
        for j in range(nblk):
            for t in tiles:
                b = gbank()
                for k in range(kch):
                    P.mm(PS[b][:, :], w[:, k, j * 128:(j + 1) * 128], hT[:, k, t * 512:(t + 1) * 512],
                         start=(k == 0), stop=(k == kch - 1))
                evac(j, t, PS[b])

    def proj_tm(w, kch, hT, blocks, ncols, evac):
        for tb in blocks:
            b = gbank()
            for k in range(kch):
                P.mm(PS[b][:, 0:ncols], hT[:, k, tb * 128:(tb + 1) * 128], w[:, k, 0:ncols],
                     start=(k == 0), stop=(k == kch - 1))
            evac(tb, PS[b])

    def rope_tables(pos_dram, ntok):
        cosT = A.alloc([ntok], F32)
        sinT = A.alloc([ntok], F32)
        m = A.mark()
        pi_ = A.alloc([ntok], I32)
        tf = A.alloc([ntok], F32)
        t2 = A.alloc([ntok], F32)
        ni = A.alloc([ntok], I32)
        nf = A.alloc([ntok], F32)
        P.dma("sp", pi_, pos_dram.partition_broadcast(128))
        P.copy("dve", tf, pi_)
        P.ts("dve", tf, tf, ppc("invf"), 1.0 / (2.0 * math.pi), op0=ALU.mult, op1=ALU.mult)
        P.copy("dve", ni, tf)
        P.copy("dve", nf, ni)
        P.tt("dve", t2, tf, nf, ALU.subtract)
        P.act(sinT, t2, AF.Sin, scale=2.0 * math.pi)
        P.ts("dve", tf, tf, 0.25, None, op0=ALU.add)
        P.copy("dve", ni, tf)
        P.copy("dve", nf, ni)
        P.tt("dve", t2, tf, nf, ALU.subtract)
        P.act(cosT, t2, AF.Sin, scale=2.0 * math.pi)
        A.release(m)
        return cosT, sinT

    def qk_evac_factory(dst, gname, cosT, sinT, is_q, scr):
        pend = []

        def part_b(jglob, t, i):
            sq, xg, xgb, rs, t1, t2 = (scr[k][i] for k in ("sq", "xg", "xgb", "rs", "t1", "t2"))
            sl = slice(t * 512, (t + 1) * 512)
            P.mm(PS[6][:, :], blockones_b, sq)
            P.mm(PS[7][:, :], rotm_b, xgb)
            if is_q:
                P.act(rs, PS[6][:, :], AF.Sqrt, bias=eps64_col[:, 0:1], scale=1.0)
            else:
                P.act(rs, PS[6][:, :], AF.Sqrt, bias=eps_col[:, 0:1], scale=1.0 / 64)
            P.recip(rs, rs)
            P.tt("dve", t1, xg, cosT[:, sl], ALU.mult)
            P.tt("dve", t2, PS[7][:, :], sinT[:, sl], ALU.mult)
            P.tt("dve", t1, t1, t2, ALU.add)
            P.tt("dve", dst[:, jglob, sl], t1, rs, ALU.mult)

        def evac(jglob, t, ps):
            i = scr["i"] % 2
            scr["i"] += 1
            sq, xg, xgb = (scr[k][i] for k in ("sq", "xg", "xgb"))
            P.act(sq, ps[:, :], AF.Square)
            P.act(xg, ps[:, :], AF.Copy, scale=ppc(gname))
            P.act(xgb, ps[:, :], AF.Copy, scale=ppc(gname))
            if pend:
                part_b(*pend.pop(0))
            pend.append((jglob, t, i))

        def flush():
            while pend:
                part_b(*pend.pop(0))
        evac.flush = flush
        return evac

    def qk_scratch():
        return {"i": 0,
                "sq": [A.alloc([512], BF16) for _ in range(2)],
                "xg": [A.alloc([512], F32) for _ in range(2)],
                "xgb": [A.alloc([512], BF16) for _ in range(2)],
                "rs": [A.alloc([512], F32) for _ in range(2)],
                "t1": [A.alloc([512], F32) for _ in range(2)],
                "t2": [A.alloc([512], F32) for _ in range(2)]}

    def v_evac_factory(vpad, gi, scale):
        def evac(tb, ps):
            src = ps[:, :].rearrange("p (a h e) -> p a h e", a=4, h=2)
            dst = vpad[:, tb, 4 * gi:4 * gi + 4, :].rearrange("p a (h e) -> p a h e", h=3)
            for hh in range(2):
                P.act(dst[:, :, 2 * hh, :], src[:, :, hh, :], AF.Copy, scale=scale)
        return evac

    def conv_chunk(raw, ntok, chunk, out_bf, acc):
        o, _ = PP["convw"]
        w = lambda j: pp[:, o + chunk * 4 + j:o + chunk * 4 + j + 1]
        P.ts("dve", acc, raw[:, 0:ntok], w(0), ppc("convb", chunk), op0=ALU.mult, op1=ALU.add)
        for j in range(1, 4):
            P.stt("dve", acc, raw[:, j:j + ntok], w(j), acc, ALU.mult, ALU.add)
        P.act(out_bf, acc, AF.Silu)

    def transpose_to_tok(srcT, nblk, dst_view):
        b = gbank()
        pv = PSB[b].rearrange("p (a c) -> p a c", a=8)
        for i in range(nblk):
            P.tr(pv[:, i, :], srcT[:, i * 128:(i + 1) * 128], ident_b)
        P.act(dst_view, pv[:, 0:nblk, :], AF.Copy)

    def dt_block(hT, nblk, dt_out):
        b = gbank()
        pv = PS[b][:, 0:nblk * 16].rearrange("p (a c) -> p a c", a=nblk)
        for tb in range(nblk):
            for k in range(16):
                P.mm(pv[:, tb, :], hT[:, k, tb * 128:(tb + 1) * 128], wdt_b[:, k, :], start=(k == 0), stop=(k == 15))
        P.tt("dve", dt_out, pv, dtb_row.unsqueeze(1).to_broadcast([128, nblk, 16]), ALU.add)
        P.act(dt_out, dt_out, AF.Exp)
        P.act(dt_out, dt_out, AF.Ln, bias=1.0)

    def ssd_small_all(dt_all, sm):
        fl = lambda a: a.rearrange("p c h -> p (c h)")
        P.tt("dve", sm["dtA"], dt_all, arow.unsqueeze(1).to_broadcast([128, 8, 16]), ALU.mult)
        P.mm(PS[0][:, 0:128], tri_f, fl(sm["dtA"]))
        P.mm(PS[1][:, 0:128], ones_f, fl(sm["dtA"]))
        P.copy("dve", fl(sm["acs"]), PS[0][:, 0:128])
        P.copy("dve", fl(sm["totb"]), PS[1][:, 0:128])
        P.ts("dve", fl(sm["nacs"]), fl(sm["acs"]), -1.0, None, op0=ALU.mult)
        P.act(fl(sm["cdb"]), fl(sm["totb"]), AF.Exp)
        P.tt("dve", fl(sm["ds"]), fl(sm["totb"]), fl(sm["acs"]), ALU.subtract)
        P.act(fl(sm["ds"]), fl(sm["ds"]), AF.Exp)
        P.tt("dve", fl(sm["dsdt"]), fl(sm["ds"]), fl(dt_all), ALU.mult)
        P.act(fl(sm["eacs"]), fl(sm["acs"]), AF.Exp)

    def ssd_small_alloc():
        return {k: A.alloc([8, 16], F32) for k in ("dtA", "acs", "nacs", "totb", "cdb", "ds", "dsdt", "eacs")}

    def ssd_state_update(x_tok_blk, B_tok_blk, sm, cb, state, xw, tmp):
        P.tt("dve", xw.rearrange("p (h e) -> p h e", h=16), x_tok_blk.rearrange("p (h e) -> p h e", h=16),
             sm["dsdt"][:, cb, :].unsqueeze(2).to_broadcast([128, 16, 64]), ALU.mult)
        for half in range(2):
            for gg in range(2):
                g = half * 2 + gg
                P.mm(PS[2 + half][:, gg * 256:(gg + 1) * 256], B_tok_blk[:, g * 128:(g + 1) * 128], xw[:, g * 256:(g + 1) * 256])
        P.tt("dve", tmp.rearrange("p (h e) -> p h e", h=16), state.rearrange("p (h e) -> p h e", h=16),
             sm["cdb"][:, cb, :].unsqueeze(2).to_broadcast([128, 16, 64]), ALU.mult)
        for half in range(2):
            P.tt("dve", state[:, half * 512:(half + 1) * 512], tmp[:, half * 512:(half + 1) * 512], PS[2 + half][:, :], ALU.add)

    state = A.alloc([1024], F32)
    tail = A.alloc([16, 3], F32)
    P.memset("dve", state, 0.0)
    hbuf = A.alloc([16, NT], BF16)
    m_hbuf = A.mark()
    kT_ctx = A.alloc([8, NT], BF16)
    vpad_ctx_f = A.alloc([8 * 8 * 192], BF16)
    vpad_ctx = vpad_ctx_f.rearrange("p (a b c) -> p a b c", a=8, b=8)
    P.memset("dve", vpad_ctx_f, 1.0)
    P.ts("dve", vpad_ctx_f, vpad_ctx_f, flag[:, 0:1], None, op0=ALU.mult)
    m_carry = A.mark()

    hT = hbuf
    norm_stream(d["xT_ctx"], "gmix", hT)
    tap("h_ctx", hT)
    x_tok = A.alloc([8, 1024], BF16)
    B_tok = A.alloc([8, 512], BF16)
    dt_c = A.alloc([8, 16], F32)
    m1 = A.mark()
    cosT, sinT = rope_tables(d["pos_ctx"], NT)
    scr = qk_scratch()
    for gi in range(2):
        w = load_w(wsl("w_a", 0, 2048, WK["k"] + gi * 512, WK["k"] + (gi + 1) * 512), 16, 512)
        ev = qk_evac_factory(kT_ctx, "gk2", cosT, sinT, False, scr)
        proj_fm(w, 16, hT, [0, 1], 4, lambda j, t, ps, gi=gi, ev=ev: ev(gi * 4 + j, t, ps))
        ev.flush()
    tap("k_ctx", kT_ctx)
    A.release(m1)
    start_gather(["w_b"])
    for gi in range(2):
        w = load_w(wsl("w_a", 0, 2048, WK["v"] + gi * 512, WK["v"] + (gi + 1) * 512), 16, 512)
        proj_tm(w, 16, hT, range(8), 512, v_evac_factory(vpad_ctx, gi, flag[:, 0:1]))
    start_gather(["w_out", "w_ckv", "w_cq", "w_co"])
    raw = [A.alloc([3 + NT], F32) for _ in range(2)]
    acc = A.alloc([NT], F32)
    cT = [A.alloc([NT], BF16) for _ in range(2)]
    ci = {"i": 0}
    trq = []

    def xbc_group_ctx(col0, nblk, chunk0, kind):
        w = load_w(wsl("w_a", 0, 2048, col0, col0 + nblk * 128), 16, nblk * 128)
        for j in range(nblk):
            r = raw[ci["i"] % 2]
            ct = cT[ci["i"] % 2]
            ci["i"] += 1
            chunk = chunk0 + j
            tiles = [0, 1] if kind != "C" else [1]
            P.memset("dve", r[:, 0:3], 0.0)
            for t in tiles:
                b = gbank()
                for k in range(16):
                    P.mm(PS[b][:, :], w[:, k, j * 128:(j + 1) * 128], hT[:, k, t * 512:(t + 1) * 512], start=(k == 0), stop=(k == 15))
                P.act(r[:, 3 + t * 512:3 + (t + 1) * 512], PS[b][:, :], AF.Copy)
            P.copy("dve", tail[:, chunk, :], r[:, NT:NT + 3])
            if kind == "C":
                continue
            conv_chunk(r, NT, chunk, ct, acc)
            while trq:
                transpose_to_tok(*trq.pop(0))
            if kind == "x":
                trq.append((ct, 8, x_tok[:, :, chunk * 128:(chunk + 1) * 128]))
            else:
                trq.append((ct, 8, B_tok[:, :, (chunk - 8) * 128:(chunk - 7) * 128]))
    xbc_group_ctx(WA["x"], 4, 0, "x")
    xbc_group_ctx(WA["x"] + 512, 4, 4, "x")
    xbc_group_ctx(WA["B"], 4, 8, "B")
    xbc_group_ctx(WA["C"], 4, 12, "C")
    while trq:
        transpose_to_tok(*trq.pop(0))
    start_gather(["w_up", "w_down"])
    dt_block(hT, 8, dt_c)
    tap("dt_ctx", dt_c)
    sm = ssd_small_alloc()
    xw = A.alloc([1024], BF16)
    tmp = A.alloc([1024], F32)
    ssd_small_all(dt_c, sm)
    for cb in range(8):
        ssd_state_update(x_tok[:, cb, :], B_tok[:, cb, :], sm, cb, state, xw, tmp)
    P.ts("dve", state, state, flag[:, 0:1], None, op0=ALU.mult)
    tap("state", state)
    A.release(m_carry)
    if stop == "C":
        return A.hi

    hT = hbuf
    norm_stream(d["xT_own"], "gmix", hT)
    tap("h_own", hT)
    qT = A.alloc([8, NT], BF16)
    kT = A.alloc([8, NT], BF16)
    vpad_f = A.alloc([8 * 8 * 192], BF16)
    vpad = vpad_f.rearrange("p (a b c) -> p a b c", a=8, b=8)
    P.memset("dve", vpad_f, 1.0)
    m1 = A.mark()
    cosT, sinT = rope_tables(d["pos_own"], NT)
    scr = qk_scratch()
    for gi in range(2):
        w = load_w(wsl("w_b", 0, 2048, WB["q"] + gi * 512, WB["q"] + (gi + 1) * 512), 16, 512)
        ev = qk_evac_factory(qT, "gq2", cosT, sinT, True, scr)
        proj_fm(w, 16, hT, [0, 1], 4, lambda j, t, ps, gi=gi, ev=ev: ev(gi * 4 + j, t, ps))
        ev.flush()
    for gi in range(2):
        w = load_w(wsl("w_a", 0, 2048, WK["k"] + gi * 512, WK["k"] + (gi + 1) * 512), 16, 512)
        ev = qk_evac_factory(kT, "gk2", cosT, sinT, False, scr)
        proj_fm(w, 16, hT, [0, 1], 4, lambda j, t, ps, gi=gi, ev=ev: ev(gi * 4 + j, t, ps))
        ev.flush()
    A.release(m1)
    for gi in range(2):
        w = load_w(wsl("w_a", 0, 2048, WK["v"] + gi * 512, WK["v"] + (gi + 1) * 512), 16, 512)
        proj_tm(w, 16, hT, range(8), 512, v_evac_factory(vpad, gi, 1.0))
    tap("q_own", qT)
    tap("k_own", kT)
    if stop == "qkv":
        return A.hi
    mixT = hbuf
    pT = [A.alloc([512], BF16) for _ in range(6)]
    araw = A.alloc([8, 512], F32)
    rden = A.alloc([512], F32)
    rsw = A.alloc([512], F32)
    sqa = [A.alloc([512], BF16) for _ in range(2)]
    rsa = A.alloc([512], F32)
    LAG = 3
    for qt in range(2):
        q0 = NT + qt * 512
        nkb = (q0 + 512) // 128
        pairs = []
        for hp in range(8):
            for kb in range(nkb):
                pairs.append((hp, kb))
        pend = []

        def stage1(pr, idx):
            hp, kb = pr
            pts, sops, banks = [], [], []
            for hh in range(2):
                if kb < 8:
                    ksrc = kT_ctx[64 * hh:64 * hh + 64, hp, kb * 128:(kb + 1) * 128]
                else:
                    ksrc = kT[64 * hh:64 * hh + 64, hp, (kb - 8) * 128:(kb - 7) * 128]
                sb_ = gbank()
                banks.append(sb_)
                sops.append(P.mm(PS[sb_][:, :], ksrc, qT[64 * hh:64 * hh + 64, hp, qt * 512:(qt + 1) * 512]))
            sops[0].deps |= (sops[1].deps - {sops[0]})
            j0 = q0 - kb * 128 + MOFF
            for hh in range(2):
                pt = pT[(2 * idx + hh) % 6]
                P.act(pt, PS[banks[hh]][:, :], AF.Exp, bias=negb_attn[:, 0:1])
                pts.append(pt)
            for hh in range(2):
                P.tt("dve", pts[hh], pts[hh], cmask_b[:, j0:j0 + 512], ALU.mult)
            return pts

        def stage2(pr, pts):
            hp, kb = pr
            ob, db = (4, 5) if (hp % 2 == 0) else (6, 7)
            oops = []
            for hh in range(2):
                if kb < 8:
                    vsrc = vpad_ctx[:, kb, hp, 64 * hh:64 * hh + 128]
                else:
                    vsrc = vpad[:, kb - 8, hp, 64 * hh:64 * hh + 128]
                bank = ob if hh == 0 else db
                oops.append(P.mm(PS[bank][:, :], vsrc, pts[hh], start=(kb == 0), stop=(kb == nkb - 1)))
            oops[0].deps |= (oops[1].deps - {oops[0]})
            if kb == nkb - 1:
                P.recip(rden[64:128, :], PS[ob][64:128, :])
                P.recip(rden[0:64, :], PS[db][0:64, :])
                sbk = gbank()
                P.mm(PS[sbk][:, :], swap_f, rden)
                P.act(rsw, PS[sbk][:, :], AF.Copy)
                P.tt("dve", araw[0:64, hp, :], PS[ob][0:64, :], rsw[0:64, :], ALU.mult)
                P.tt("dve", araw[64:128, hp, :], PS[db][64:128, :], rsw[64:128, :], ALU.mult)

        for idx, pr in enumerate(pairs):
            pend.append((pr, stage1(pr, idx)))
            if len(pend) > 2:
                stage2(*pend.pop(0))
        while pend:
            stage2(*pend.pop(0))
        tap("attn_raw%d" % qt, araw)
        for hp in range(8):
            P.act(sqa[hp % 2], araw[:, hp, :], AF.Square)
            P.mm(PS[4][:, :], ones_b, sqa[hp % 2], start=(hp == 0), stop=(hp == 7))
        P.act(rsa, PS[4][:, :], AF.Sqrt, bias=eps_col[:, 0:1], scale=1.0 / 1024)
        P.recip(rsa, rsa)
        for hp in range(8):
            P.stt("dve", mixT[:, hp, qt * 512:(qt + 1) * 512], araw[:, hp, :], ppc("gattn", hp), rsa, ALU.mult, ALU.mult)
    A.release(m_hbuf)
    if stop == "attn":
        tap("mixT", mixT)
        return A.hi

    sz = A.alloc([8, 1024], F32)
    x_tok = A.alloc([8, 1024], BF16)
    B_tok = A.alloc([8, 512], BF16)
    BT = A.alloc([4, NT], BF16)
    CT = A.alloc([4, NT], BF16)
    dt_o = A.alloc([8, 16], F32)
    m2 = A.mark()
    hT = A.alloc([16, NT], BF16)
    norm_stream(d["xT_own"], "gmix", hT)
    for gi in range(2):
        w = load_w(wsl("w_b", 0, 2048, WB["z"] + gi * 512, WB["z"] + (gi + 1) * 512), 16, 512)
        proj_tm(w, 16, hT, range(8), 512,
                lambda tb, ps, gi=gi: P.act(sz[:, tb, gi * 512:(gi + 1) * 512], ps[:, :], AF.Silu))
    raw = [A.alloc([3 + NT], F32) for _ in range(2)]
    acc = A.alloc([NT], F32)
    cT = [A.alloc([NT], BF16) for _ in range(2)]
    ci = {"i": 0}
    trq = []

    def xbc_group_own(col0, chunk0, kind):
        w = load_w(wsl("w_a", 0, 2048, col0, col0 + 512), 16, 512)
        for j in range(4):
            r = raw[ci["i"] % 2]
            ct = cT[ci["i"] % 2]
            ci["i"] += 1
            chunk = chunk0 + j
            P.copy("dve", r[:, 0:3], tail[:, chunk, :])
            for t in range(2):
                b = gbank()
                for k in range(16):
                    P.mm(PS[b][:, :], w[:, k, j * 128:(j + 1) * 128], hT[:, k, t * 512:(t + 1) * 512], start=(k == 0), stop=(k == 15))
                P.act(r[:, 3 + t * 512:3 + (t + 1) * 512], PS[b][:, :], AF.Copy)
            if kind == "x":
                conv_chunk(r, NT, chunk, ct, acc)
            elif kind == "B":
                conv_chunk(r, NT, chunk, BT[:, chunk - 8, :], acc)
            else:
                conv_chunk(r, NT, chunk, CT[:, chunk - 12, :], acc)
            while trq:
                transpose_to_tok(*trq.pop(0))
            if kind == "x":
                trq.append((ct, 8, x_tok[:, :, chunk * 128:(chunk + 1) * 128]))
            elif kind == "B":
                trq.append((BT[:, chunk - 8, :], 8, B_tok[:, :, (chunk - 8) * 128:(chunk - 7) * 128]))
    xbc_group_own(WA["x"], 0, "x")
    xbc_group_own(WA["x"] + 512, 4, "x")
    xbc_group_own(WA["B"], 8, "B")
    xbc_group_own(WA["C"], 12, "C")
    while trq:
        transpose_to_tok(*trq.pop(0))
    dt_block(hT, 8, dt_o)
    tap("x_tok", x_tok)
    tap("dt_own", dt_o)
    A.release(m2)
    sm = ssd_small_alloc()
    xw = A.alloc([1024], BF16)
    tmp = A.alloc([1024], F32)
    state_b = A.alloc([1024], BF16)
    LT = [A.alloc([128], F32) for _ in range(6)]
    WT = [A.alloc([128], BF16) for _ in range(6)]
    y1 = A.alloc([1024], F32)
    y2 = A.alloc([1024], F32)
    ysq = A.alloc([1024], F32)
    ss4 = A.alloc([4], F32)
    ynb = A.alloc([1024], BF16)
    ssd_small_all(dt_o, sm)
    for cb in range(8):
        blk = slice(cb * 128, (cb + 1) * 128)
        P.copy("dve", state_b, state)
        for g in range(4):
            P.mm(PS[1][:, g * 128:(g + 1) * 128], BT[:, g, blk], CT[:, g, blk])

        def s1(h):
            g = h // 4
            lb = 2 + h % 2
            reg = PS[lb][:, ((h // 2) % 4) * 128:((h // 2) % 4 + 1) * 128]
            P.mm(reg, sm["dtA"][:, cb, h:h + 1].to_broadcast([128, 128]), tri_f, start=True, stop=False)
            P.mm(reg, ident_f, negmask_f, start=False, stop=True)
            lt = LT[h % 6]
            P.act(lt, reg, AF.Exp, bias=sm["nacs"][:, cb, h:h + 1])
            wt = WT[h % 6]
            P.stt("dve", wt, lt, dt_o[:, cb, h:h + 1], PS[1][:, g * 128:(g + 1) * 128], ALU.mult, ALU.mult)
            return wt

        def s2(h, wt):
            yb = 4 + h // 8
            P.mm(PS[yb][:, (h % 8) * 64:(h % 8 + 1) * 64], wt, x_tok[:, cb, h * 64:(h + 1) * 64])
        pend = []
        for h in range(16):
            pend.append((h, s1(h)))
            if len(pend) > 3:
                s2(*pend.pop(0))
        while pend:
            s2(*pend.pop(0))
        for half in range(2):
            for gg in range(2):
                g = half * 2 + gg
                P.mm(PS[6 + half][:, gg * 256:(gg + 1) * 256], CT[:, g, blk], state_b[:, g * 256:(g + 1) * 256])
        for half in range(2):
            hs = slice(half * 512, (half + 1) * 512)
            P.tt("dve", y1[:, hs].rearrange("p (h e) -> p h e", h=8), PS[6 + half][:, :].rearrange("p (h e) -> p h e", h=8),
                 sm["eacs"][:, cb, half * 8:(half + 1) * 8].unsqueeze(2).to_broadcast([128, 8, 64]), ALU.mult)
            P.tt("dve", y1[:, hs], y1[:, hs], PS[4 + half][:, :], ALU.add)
        P.tt("dve", y2.rearrange("p (h e) -> p h e", h=16), x_tok[:, cb, :].rearrange("p (h e) -> p h e", h=16),
             dskip_row.unsqueeze(2).to_broadcast([128, 16, 64]), ALU.mult)
        P.tt("dve", y1, y1, y2, ALU.add)
        if cb == 0:
            tap("ssm_y0", y1)
        P.tt("dve", y1, y1, sz[:, cb, :], ALU.mult)
        P.tt("dve", ysq, y1, y1, ALU.mult)
        P.reduce("dve", ss4, ysq.rearrange("p (g e) -> p g e", g=4), ALU.add)
        P.act(ss4, ss4, AF.Sqrt, bias=eps_col[:, 0:1], scale=1.0 / 256)
        P.recip(ss4, ss4)
        P.tt("dve", ynb.rearrange("p (g e) -> p g e", g=4), y1.rearrange("p (g e) -> p g e", g=4),
             ss4.unsqueeze(2).to_broadcast([128, 4, 256]), ALU.mult)
        tb_ = gbank()
        pv = PSB[tb_].rearrange("p (a c) -> p a c", a=8)
        for c in range(8):
            P.tr(pv[:, c, :], ynb[:, c * 128:(c + 1) * 128], ident_b)
        o, _ = PP["gssm"]
        P.tt("dve", mixT[:, 8:16, blk], pv, pp[:, o:o + 8].unsqueeze(2).to_broadcast([128, 8, 128]), ALU.mult)
        ssd_state_update(x_tok[:, cb, :], B_tok[:, cb, :], sm, cb, state, xw, tmp)
    tap("mixT", mixT)
    A.release(m_hbuf)
    if stop == "ssd":
        return A.hi

    xT = A.alloc([16, NT], F32)
    xv = d["xT_own"].rearrange("(c p) n -> p c n", p=128)
    for c4 in range(4):
        P.dma("sp", xT[:, c4 * 4:(c4 + 1) * 4, :], xv[:, c4 * 4:(c4 + 1) * 4, :])
    m_x = A.mark()

    def resid_evac(j0):
        def evac(j, t, ps):
            sl = slice(t * 512, (t + 1) * 512)
            P.tt("dve", xT[:, j0 + j, sl], xT[:, j0 + j, sl], ps[:, :], ALU.add)
        return evac
    for gi in range(4):
        w = load_w(wsl("w_out", 0, 2048, gi * 512, (gi + 1) * 512), 16, 512)
        proj_fm(w, 16, mixT, [0, 1], 4, resid_evac(gi * 4))
    tap("x1", xT)

    if stop == "wout":
        return A.hi
    A.release(m_x)
    h2 = hbuf
    norm_resident(xT, "gcross", h2)
    memh = A.alloc([16, 256], BF16)
    norm_stream(d["memT"], "gmem", memh, ntok=256)
    kcT = A.alloc([4, 256], BF16)
    vc = A.alloc([2, 512], BF16)
    qcT = A.alloc([4, NT], BF16)
    ocT = A.alloc([4, NT], BF16)
    sqc = [A.alloc([512], BF16) for _ in range(2)]
    rsc = [A.alloc([512], F32) for _ in range(2)]
    cci = {"i": 0}

    def cnorm_evac(dst, gname, bias_col, scale, ntok):
        def evac(j, t, ps):
            i = cci["i"] % 2
            cci["i"] += 1
            n = min(512, ntok)
            sl = slice(t * 512, t * 512 + n)
            P.act(sqc[i][:, 0:n], ps[:, 0:n], AF.Square)
            P.mm(PS[6][:, 0:n], ones_b, sqc[i][:, 0:n])
            P.act(rsc[i][:, 0:n], PS[6][:, 0:n], AF.Sqrt, bias=bias_col, scale=scale)
            P.recip(rsc[i][:, 0:n], rsc[i][:, 0:n])
            P.stt("dve", dst[:, j, sl], ps[:, 0:n], ppc(gname), rsc[i][:, 0:n], ALU.mult, ALU.mult)
        return evac
    w = load_w(wsl("w_ckv", 0, 2048, 0, 512), 16, 512)
    ev = cnorm_evac(kcT, "gck", eps_col[:, 0:1], 1.0 / 128, 256)
    for j in range(4):
        b = gbank()
        for k in range(16):
            P.mm(PS[b][:, 0:256], w[:, k, j * 128:(j + 1) * 128], memh[:, k, :], start=(k == 0), stop=(k == 15))
        ev(j, 0, PS[b])
    w = load_w(wsl("w_ckv", 0, 2048, 512, 1024), 16, 512)
    proj_tm(w, 16, memh, range(2), 512, lambda tb, ps: P.act(vc[:, tb, :], ps[:, :], AF.Copy))
    w = load_w(wsl("w_cq", 0, 2048, 0, 512), 16, 512)
    proj_fm(w, 16, h2, [0, 1], 4, cnorm_evac(qcT, "gcq", eps128_col[:, 0:1], 1.0, NT))
    ptc = [A.alloc([512], BF16) for _ in range(3)]
    rdc = A.alloc([512], F32)
    pci = 0
    for t in range(2):
        sl = slice(t * 512, (t + 1) * 512)
        for hd in range(4):
            ob, db = (4, 5) if (hd % 2 == 0) else (6, 7)
            for mb in range(2):
                sb_ = gbank()
                P.mm(PS[sb_][:, :], kcT[:, hd, mb * 128:(mb + 1) * 128], qcT[:, hd, sl])
                pt = ptc[pci % 3]
                pci += 1
                P.act(pt, PS[sb_][:, :], AF.Exp, bias=negb_cross[:, 0:1])
                P.mm(PS[ob][:, :], vc[:, mb, hd * 128:(hd + 1) * 128], pt, start=(mb == 0), stop=(mb == 1))
                P.mm(PS[db][:, :], ones_b, pt, start=(mb == 0), stop=(mb == 1))
            P.recip(rdc, PS[db][:, :])
            P.tt("dve", ocT[:, hd, sl], PS[ob][:, :], rdc, ALU.mult)
    tap("ocT", ocT)
    w = load_w(wsl("w_co", 0, 512, 0, 2048), 4, 2048)
    for j in range(16):
        for t in range(2):
            b = gbank()
            for k in range(4):
                P.mm(PS[b][:, :], w[:, k, j * 128:(j + 1) * 128], ocT[:, k, t * 512:(t + 1) * 512], start=(k == 0), stop=(k == 3))
            sl = slice(t * 512, (t + 1) * 512)
            P.tt("dve", xT[:, j, sl], xT[:, j, sl], PS[b][:, :], ALU.add)
    tap("x2", xT)

    if stop == "cross":
        return A.hi
    A.release(m_x)
    h3 = hbuf
    norm_resident(xT, "gmlp", h3)
    uT = [A.alloc([4, NT], BF16) for _ in range(2)]
    rl = [A.alloc([512], F32) for _ in range(2)]
    ri = 0
    gb_state["lst"] = [0, 1, 2, 3, 4, 5, 6, 7]
    for fg in range(16):
        wu = load_w(wsl("w_up", 0, 2048, fg * 512, (fg + 1) * 512), 16, 512)
        wd = load_w(wsl("w_down", fg * 512, (fg + 1) * 512, 0, 2048), 4, 2048)
        u = uT[fg % 2]
        for j in range(4):
            for t in range(2):
                b = gbank()
                for k in range(16):
                    P.mm(PS[b][:, :], wu[:, k, j * 128:(j + 1) * 128], h3[:, k, t * 512:(t + 1) * 512], start=(k == 0), stop=(k == 15))
                r = rl[ri % 2]
                ri += 1
                P.act(r, PS[b][:, :], AF.Relu)
                P.act(u[:, j, t * 512:(t + 1) * 512], r, AF.Square)
        for j in range(16):
            for t in range(2):
                b = gbank()
                for k in range(4):
                    P.mm(PS[b][:, :], wd[:, k, j * 128:(j + 1) * 128], u[:, k, t * 512:(t + 1) * 512], start=(k == 0), stop=(k == 3))
                sl = slice(t * 512, (t + 1) * 512)
                P.tt("dve", xT[:, j, sl], xT[:, j, sl], PS[b][:, :], ALU.add)
    ov = d["outT"].rearrange("(c p) n -> p c n", p=128)
    for c4 in range(4):
        P.dma("sp", ov[:, c4 * 4:(c4 + 1) * 4, :], xT[:, c4 * 4:(c4 + 1) * 4, :], is_output=True)
    P.wait_all("sp", P.out_tokens)
    return A.hi


from concourse.bass_utils import run_bass_kernel_spmd

W_NAMES = ("w_out", "w_cq", "w_ckv", "w_co", "w_up", "w_down")
GATHER = True
ROW_NAMES = ("dt_bias", "a_log", "d_skip", "g_q", "g_k", "g_cq", "g_ck")


def core_inputs(inp, b, half, consts, ppk, wts, core=0, gather=False):
    x = inp["x"][b]
    pos = inp["positions"][b]
    own = x[half * NT:(half + 1) * NT]
    if half == 1:
        ctx = x[0:NT]
        pctx = pos[0:NT]
    else:
        ctx = np.zeros_like(own)
        pctx = np.zeros(NT, dtype=pos.dtype)
    m = {
        "xT_own": np.ascontiguousarray(own.T),
        "xT_ctx": np.ascontiguousarray(ctx.T),
        "memT": np.ascontiguousarray(inp["mem"][b].T),
        "pos_own": np.ascontiguousarray(pos[half * NT:(half + 1) * NT][None, :]).astype(np.int32),
        "pos_ctx": np.ascontiguousarray(pctx[None, :]).astype(np.int32),
        "flag": np.full((128, 1), float(half), np.float32),
        "pp": ppk,
    }
    m.update(consts)
    for n, w in wts.items():
        if gather and n.startswith("w_"):
            r = w.shape[0] // 8
            m[n] = np.ascontiguousarray(w[core * r:(core + 1) * r])
        else:
            m[n] = w
    return m


def prep_shared(inp):
    consts = host_consts()
    ppk = host_pp(inp)
    wts = {n: np.ascontiguousarray(inp[n][0]) for n in W_NAMES}
    w_in = inp["w_in"][0]
    wts["w_a"] = np.ascontiguousarray(np.concatenate([w_in[:, 1024:3072], w_in[:, 4096:6160]], axis=1))
    wts["w_b"] = np.ascontiguousarray(np.concatenate([w_in[:, 0:1024], w_in[:, 3072:4096]], axis=1))
    for n in ROW_NAMES:
        wts[n] = np.ascontiguousarray(inp[n][0][None, :])
    return consts, ppk, wts


def declare(nc, sample, outs):
    d = {}
    for k, v in sample.items():
        dt = I32 if v.dtype == np.int32 else F32
        d[k] = nc.dram_tensor(k, list(v.shape), dt, kind="ExternalInput").ap()
    for k, shape in outs.items():
        d[k] = nc.dram_tensor(k, list(shape), F32, kind="ExternalOutput").ap()
    return d


def kernel(**inputs):
    inp = {k: np.asarray(v) for k, v in inputs.items()}
    consts, ppk, wts = prep_shared(inp)
    in_maps = []
    for c in range(8):
        in_maps.append(core_inputs(inp, c // 2, c % 2, consts, ppk, wts, core=c, gather=GATHER))
    nc = bass.Bass("TRN2", target_bir_lowering=False)
    P = P2(nc)
    d = declare(nc, in_maps[0], {"outT": [D, NT]})
    build(P, d, gather=GATHER)
    P.emit()
    res = run_bass_kernel_spmd(nc, in_maps, core_ids=list(range(8)))
    out = np.empty((4, 2048, D), np.float32)
    for c in range(8):
        out[c // 2, (c % 2) * NT:(c % 2 + 1) * NT, :] = res.results[c]["outT"].T
    return out
```
